# Optimizing a Trainium2 kernel written in Bass

```python
import jax, jax.numpy as jnp
from jax import lax
import numpy as np

D_MODEL = 1024
BATCH = 16
SEQ = 2048
DEPTH = 2

D_MIX = D_MODEL
HEAD_DIM = 64
ATTN_WIDTH = D_MIX // 2
N_ATTN_HEADS = ATTN_WIDTH // HEAD_DIM
GCONV_WIDTH = D_MIX // 4
CCONV_WIDTH = D_MIX - ATTN_WIDTH - GCONV_WIDTH
DILATED_PAIRS = ((128, 1), (512, 4), (2048, 16))
ATTN_BLOCK = 128
GCONV_K = 3
CCONV_K = 31
D_FF = 11 * D_MODEL // 4
IN_SPLITS = (ATTN_WIDTH, ATTN_WIDTH, ATTN_WIDTH,
             GCONV_WIDTH, GCONV_WIDTH, GCONV_WIDTH,
             CCONV_WIDTH, CCONV_WIDTH)
D_IN = sum(IN_SPLITS)
RMS_EPS = 1e-6
LN_EPS = 1e-5

kernel_name = "hybrid_dilated_attn_gconv_conformer_macaron"


def rms_norm(x, g):
    xf = x.astype(jnp.float32)
    y = xf * lax.rsqrt(jnp.mean(xf * xf, axis=-1, keepdims=True) + RMS_EPS)
    return (y * g.astype(jnp.float32)).astype(x.dtype)


def layer_norm(x, g, b):
    xf = x.astype(jnp.float32)
    mu = jnp.mean(xf, axis=-1, keepdims=True)
    var = jnp.mean(jnp.square(xf - mu), axis=-1, keepdims=True)
    y = (xf - mu) * lax.rsqrt(var + LN_EPS)
    return (y * g.astype(jnp.float32) + b.astype(jnp.float32)).astype(x.dtype)


def swiglu(h, wg, wu, wd):
    return (jax.nn.silu(h @ wg) * (h @ wu)) @ wd


def causal_depthwise_conv(u, w):
    kw, c = w.shape
    return lax.conv_general_dilated(
        u, w[:, None, :], window_strides=(1,), padding=[(kw - 1, 0)],
        dimension_numbers=('NWC', 'WIO', 'NWC'), feature_group_count=c)


def alibi_slopes(n_heads):
    return jnp.exp2(-8.0 * jnp.arange(1, n_heads + 1, dtype=jnp.float32) / n_heads)


def dilated_branch(q, k, v, dil, n_steps, slopes):
    b, s, h, e = q.shape
    blk = ATTN_BLOCK
    L = s // dil
    nb = -(-L // blk)
    Lp = nb * blk

    def strided(t):
        return t.reshape(b, L, dil, h, e).transpose(0, 2, 3, 1, 4)

    qb = jnp.pad(strided(q), ((0, 0), (0, 0), (0, 0), (0, Lp - L), (0, 0)))
    qb = qb.reshape(b, dil, h, nb, blk, e)

    def band(t):
        tp = jnp.pad(t, ((0, 0), (0, 0), (0, 0), (blk, Lp - L), (0, 0)))
        prev = tp[:, :, :, :Lp].reshape(b, dil, h, nb, blk, e)
        cur = tp[:, :, :, blk:].reshape(b, dil, h, nb, blk, e)
        return jnp.concatenate([prev, cur], axis=4)

    kb = band(strided(k))
    vb = band(strided(v))

    scores = jnp.einsum('brhnqe,brhnke->brhnqk', qb, kb).astype(jnp.float32) * (e ** -0.5)
    qi = jnp.arange(blk)[:, None]
    kj = jnp.arange(2 * blk)[None, :]
    steps = qi - kj + blk
    key_pos = jnp.arange(nb)[:, None, None] * blk - blk + kj[None]
    valid = (steps >= 0) & (steps <= n_steps) & (key_pos >= 0)
    bias = -slopes[:, None, None, None] * (steps * dil).astype(jnp.float32)
    scores = jnp.where(valid, scores + bias, -jnp.inf)
    lse = jax.nn.logsumexp(scores, axis=-1)
    p = jnp.exp(scores - lse[..., None]).astype(v.dtype)
    o = jnp.einsum('brhnqk,brhnke->brhnqe', p, vb)
    o = o.reshape(b, dil, h, Lp, e)[:, :, :, :L].transpose(0, 3, 1, 2, 4).reshape(b, s, h, e)
    lse = lse.reshape(b, dil, h, Lp)[..., :L].transpose(0, 3, 1, 2).reshape(b, s, h)
    return o, lse


def dilated_attention(q, k, v):
    slopes = alibi_slopes(q.shape[2])
    outs, lses = [], []
    for window, dil in DILATED_PAIRS:
        o, l = dilated_branch(q, k, v, dil, window // dil, slopes)
        outs.append(o)
        lses.append(l)
    wts = jax.nn.softmax(jnp.stack(lses, axis=0), axis=0)
    return jnp.sum(wts[..., None].astype(q.dtype) * jnp.stack(outs, axis=0), axis=0)


def hybrid_mixer(h, w_in, w_out, gconv_w, cconv_w, cconv_b, cln_g, cln_b):
    b, s, _ = h.shape
    z = h @ w_in
    q, k, v, g_b, g_c, g_x, c_val, c_gate = jnp.split(
        z, list(np.cumsum(IN_SPLITS)[:-1]), axis=-1)

    def heads(t):
        return t.reshape(b, s, N_ATTN_HEADS, HEAD_DIM)
    y_attn = dilated_attention(heads(q), heads(k), heads(v)).reshape(b, s, ATTN_WIDTH)

    y_gconv = g_b * causal_depthwise_conv(g_c * g_x, gconv_w)

    u = c_val * jax.nn.sigmoid(c_gate)
    u = causal_depthwise_conv(u, cconv_w) + cconv_b
    y_cconv = jax.nn.silu(layer_norm(u, cln_g, cln_b))

    return jnp.concatenate([y_attn, y_gconv, y_cconv], axis=-1) @ w_out


def setup_inputs(seed: int = 0) -> dict:
    key = jax.random.key(seed)
    ks = jax.random.split(key, 20)
    f32 = jnp.float32

    def nrm(k, shape, fan_in):
        return jax.random.normal(k, shape, f32) * (fan_in ** -0.5)

    def gain(k, shape):
        return 1.0 + 0.01 * jax.random.normal(k, shape, f32)

    return {
        "x": jax.random.normal(ks[0], (BATCH, SEQ, D_MODEL), f32),
        "w_in": nrm(ks[1], (DEPTH, D_MODEL, D_IN), D_MODEL),
        "w_out": nrm(ks[2], (DEPTH, D_MIX, D_MODEL), D_MIX),
        "gconv_w": nrm(ks[3], (DEPTH, GCONV_K, GCONV_WIDTH), GCONV_K),
        "cconv_w": nrm(ks[4], (DEPTH, CCONV_K, CCONV_WIDTH), CCONV_K),
        "cconv_b": 0.01 * jax.random.normal(ks[5], (DEPTH, CCONV_WIDTH), f32),
        "cln_g": gain(ks[6], (DEPTH, CCONV_WIDTH)),
        "cln_b": 0.01 * jax.random.normal(ks[7], (DEPTH, CCONV_WIDTH), f32),
        "ffn1_wg": nrm(ks[8], (DEPTH, D_MODEL, D_FF), D_MODEL),
        "ffn1_wu": nrm(ks[9], (DEPTH, D_MODEL, D_FF), D_MODEL),
        "ffn1_wd": nrm(ks[10], (DEPTH, D_FF, D_MODEL), D_FF),
        "ffn2_wg": nrm(ks[11], (DEPTH, D_MODEL, D_FF), D_MODEL),
        "ffn2_wu": nrm(ks[12], (DEPTH, D_MODEL, D_FF), D_MODEL),
        "ffn2_wd": nrm(ks[13], (DEPTH, D_FF, D_MODEL), D_FF),
        "norm_ffn1": gain(ks[14], (DEPTH, D_MODEL)),
        "norm_mix": gain(ks[15], (DEPTH, D_MODEL)),
        "norm_ffn2": gain(ks[16], (DEPTH, D_MODEL)),
        "norm_final": gain(ks[17], (D_MODEL,)),
    }


def reference(x, w_in, w_out, gconv_w, cconv_w, cconv_b, cln_g, cln_b,
              ffn1_wg, ffn1_wu, ffn1_wd, ffn2_wg, ffn2_wu, ffn2_wd,
              norm_ffn1, norm_mix, norm_ffn2, norm_final):
    for l in range(DEPTH):
        x = x + 0.5 * swiglu(rms_norm(x, norm_ffn1[l]), ffn1_wg[l], ffn1_wu[l], ffn1_wd[l])
        x = x + hybrid_mixer(rms_norm(x, norm_mix[l]), w_in[l], w_out[l], gconv_w[l],
                             cconv_w[l], cconv_b[l], cln_g[l], cln_b[l])
        x = x + 0.5 * swiglu(rms_norm(x, norm_ffn2[l]), ffn2_wg[l], ffn2_wu[l], ffn2_wd[l])
    return rms_norm(x, norm_final)
```

```python
import contextlib
import numpy as np
import concourse.bass as bass
import concourse.mybir as mybir
from concourse.bass_utils import run_bass_kernel_spmd

F32 = mybir.dt.float32
BF16 = mybir.dt.bfloat16
ALU = mybir.AluOpType
AF = mybir.ActivationFunctionType

D = 1024
S = 2048
DFF = 2816
DIN = 2816
NCH = 8
NFC = 22
NT = 4
TS = 512
N_CORES = 8
SEQ_PER_CORE = 2
PL = 98
NPAR = 2 * PL + 8
RMS_EPS = 1e-6
LN_EPS = 1e-5

COMPUTE = ("pe", "act", "dve", "pool")


class _Op:
    __slots__ = ("eng", "fn", "deps", "is_dma", "key", "count", "signal", "sigidx", "idx")


class Prog:
    def __init__(self, same_engine_sync=True):
        self.ops = []
        self.last_writer = {}
        self.readers = {}
        self.dma_counts = {}
        self.same_engine_sync = same_engine_sync

    def op(self, eng, fn, reads=(), writes=(), dma_key=None):
        o = _Op()
        o.idx = len(self.ops)
        o.eng = eng
        o.fn = fn
        o.is_dma = dma_key is not None
        o.key = dma_key
        o.signal = False
        o.sigidx = 0
        o.count = 0
        deps = set()
        for s in reads:
            w = self.last_writer.get(s)
            if w is not None:
                deps.add(w)
        for s in writes:
            w = self.last_writer.get(s)
            if w is not None:
                deps.add(w)
            for r in self.readers.get(s, ()):
                deps.add(r)
        for s in reads:
            self.readers.setdefault(s, []).append(o.idx)
        for s in writes:
            self.readers[s] = []
            self.last_writer[s] = o.idx
        deps.discard(o.idx)
        if o.is_dma:
            c = self.dma_counts.get(dma_key, 0) + 1
            self.dma_counts[dma_key] = c
            o.count = c
        o.deps = deps
        self.ops.append(o)
        return o.idx

    def _skip(self, do, eng):
        if do.is_dma:
            return False
        if do.eng == "pe" and eng == "pe":
            return True
        if do.eng == eng and not self.same_engine_sync:
            return True
        return False

    def emit(self, nc, final_wait_ops=()):
        ops = self.ops
        for o in ops:
            for d in o.deps:
                do = ops[d]
                if not do.is_dma and not self._skip(do, o.eng):
                    do.signal = True
        cnt = {e: 0 for e in COMPUTE}
        for o in ops:
            if not o.is_dma and o.signal:
                cnt[o.eng] += 1
                o.sigidx = cnt[o.eng]
        with contextlib.ExitStack() as st:
            sems = {e: st.enter_context(nc.semaphore("s_" + e)) for e in COMPUTE}
            dsems = {k: st.enter_context(nc.semaphore("d_" + str(k))) for k in self.dma_counts}
            block = st.enter_context(nc.Block())
            streams = {}
            for o in ops:
                streams.setdefault(o.eng, []).append(o)

            def run_stream(ename, eobj):
                waited = {}
                for o in streams.get(ename, []):
                    need = {}
                    for d in o.deps:
                        do = ops[d]
                        if do.is_dma:
                            sem, val, k = dsems[do.key], 16 * do.count, ("d", do.key)
                        else:
                            if self._skip(do, ename):
                                continue
                            sem, val, k = sems[do.eng], do.sigidx, ("c", do.eng)
                        if need.get(k, (None, 0))[1] < val:
                            need[k] = (sem, val)
                    for k, (sem, val) in need.items():
                        if waited.get(k, 0) >= val:
                            continue
                        waited[k] = val
                        eobj.wait_ge(sem, val)
                    ins = o.fn(eobj)
                    if o.is_dma:
                        ins.then_inc(dsems[o.key], 16)
                    elif o.signal:
                        ins.then_inc(sems[o.eng], 1)
                if ename == "sp":
                    for d in final_wait_ops:
                        do = ops[d]
                        eobj.wait_ge(dsems[do.key], 16 * do.count)

            @block.sync
            def _(e):
                run_stream("sp", e)

            @block.tensor
            def _(e):
                run_stream("pe", e)

            @block.scalar
            def _(e):
                run_stream("act", e)

            @block.vector
            def _(e):
                run_stream("dve", e)

            @block.gpsimd
            def _(e):
                run_stream("pool", e)


def build_program(nseq, layers, final_norm, same_engine_sync=True, phases=("ffn1", "mixer", "ffn2"),
                  mixparts=("gconv", "cconv", "attn", "wout")):
    nc = bass.Bass("TRN2", target_bir_lowering=False)
    dram = lambda n, s, k: nc.dram_tensor(n, s, F32, kind=k).ap()
    x_d = dram("x", [nseq, S, D], "ExternalInput")
    out_d = dram("out", [nseq, S, D], "ExternalOutput")
    w_in_d = dram("w_in", [2, D, DIN], "ExternalInput")
    w_out_d = dram("w_out", [2, D, D], "ExternalInput")
    wg_d = [dram("ffn1_wg", [2, D, DFF], "ExternalInput"), dram("ffn2_wg", [2, D, DFF], "ExternalInput")]
    wu_d = [dram("ffn1_wu", [2, D, DFF], "ExternalInput"), dram("ffn2_wu", [2, D, DFF], "ExternalInput")]
    wd_d = [dram("ffn1_wd", [2, DFF, D], "ExternalInput"), dram("ffn2_wd", [2, DFF, D], "ExternalInput")]
    par_d = dram("params", [128, NPAR], "ExternalInput")
    mask_d = dram("masks", [128, 12 * 256], "ExternalInput")
    ident_d = dram("ident", [128, 128], "ExternalInput")

    P = Prog(same_engine_sync=same_engine_sync)
    NWS, NWB = 3, 6
    RBYTES = 71 * 1024
    SB0 = 32 * 1024

    with contextlib.ExitStack() as st:
        sb = lambda n, s, d: st.enter_context(nc.sbuf_tensor(n, s, d))
        X = sb("X", [128, NCH, S], F32)
        H = sb("H", [128, NCH, S], BF16)
        R = sb("R", [128, RBYTES // 2], BF16)
        WS = [sb("WS%d" % i, [128, 8, 128], F32) for i in range(NWS)]
        WB = [sb("WB%d" % i, [128, 8, 128], BF16) for i in range(NWB)]
        SQ = [sb("SQ%d" % i, [128, TS], BF16) for i in range(2)]
        SG = [sb("SG%d" % i, [128, TS], F32) for i in range(2)]
        RSTD = sb("RSTD", [128, TS], F32)
        PAR = sb("PAR", [128, NPAR], F32)
        BIASB = sb("BIASB", [128, 12, 256], BF16)
        ONEA0 = sb("ONEA0", [128, 128], BF16)
        ONEB0 = sb("ONEB0", [128, 128], BF16)
        IDF = sb("IDF", [128, 128], F32)
        IDB = sb("IDB", [128, 128], BF16)
        ONEB = sb("ONEB", [128, 128], BF16)
        ONEF = sb("ONEF", [128, 128], F32)
        EPS = sb("EPS", [128, 2], F32)
        PS = [st.enter_context(nc.psum_tensor("ps%d" % i, [128, TS], F32)) for i in range(8)]

        def rg(off, nbytes):
            return ["R%d" % g for g in range(off // 1024, (off + nbytes - 1) // 1024 + 1)]

        def rb(off, n):
            return R[:, off // 2: off // 2 + n]

        def rf(off, n):
            return R[:, off // 2: off // 2 + 2 * n].bitcast(F32)

        def ps(b):
            return ["P%d" % b]

        def xs(c, tt):
            return "X_%d_%d" % (c, tt)

        def hs_(c, tt):
            return "H_%d_%d" % (c, tt)

        def tsl(tt):
            return slice(tt * TS, (tt + 1) * TS)

        cnt = {"ws": 0, "wb": 0, "sq": 0, "sg": 0, "zb": 0, "e": 0, "pt": 0, "ev": 0}

        def nxt(k, n):
            v = cnt[k] % n
            cnt[k] += 1
            return v

        P.op("sp", lambda e: e.dma_start(out=PAR[:], in_=par_d), writes=["PAR"], dma_key="cPAR")
        P.op("sp", lambda e: e.dma_start(out=IDF[:], in_=ident_d), writes=["IDF"], dma_key="cIDF")
        P.op("pool", lambda e: e.tensor_copy(out=IDB[:], in_=IDF[:]), reads=["IDF"], writes=["IDB"])
        P.op("pool", lambda e: e.memset(ONEB[:], 1.0), writes=["ONEB"])
        P.op("pool", lambda e: e.memset(ONEF[:], 1.0), writes=["ONEF"])
        P.op("pool", lambda e: e.memset(ONEA0[:], 0.0), writes=["ONEA0"])
        P.op("pool", lambda e: e.memset(ONEA0[:, 0:64], 1.0), writes=["ONEA0"])
        P.op("pool", lambda e: e.memset(ONEB0[:], 0.0), writes=["ONEB0"])
        P.op("pool", lambda e: e.memset(ONEB0[:, 64:128], 1.0), writes=["ONEB0"])
        P.op("pool", lambda e: e.memset(EPS[:, 0:1], RMS_EPS), writes=["EPS"])
        P.op("pool", lambda e: e.memset(EPS[:, 1:2], LN_EPS), writes=["EPS"])
        for q in range(3):
            i = nxt("ws", NWS)
            P.op("sp", lambda e, q=q, i=i: e.dma_start(
                out=WS[i][:].rearrange("p a b -> p (a b)"), in_=mask_d[:, q * 1024:(q + 1) * 1024]),
                writes=["WS%d" % i], dma_key="WS%d" % i)
            P.op("pool", lambda e, q=q, i=i: e.tensor_copy(
                out=BIASB[:, q * 4:(q + 1) * 4, :],
                in_=WS[i][:].rearrange("p a b -> p (a b)").rearrange("p (t c) -> p t c", t=4)),
                reads=["WS%d" % i], writes=["BIASB"])

        def wtile(W2d, k0, nk, col0):
            i = nxt("ws", NWS)
            j = nxt("wb", NWB)
            src = W2d[k0 * 128:(k0 + nk) * 128, col0:col0 + 128].rearrange("(c p) f -> p c f", p=128)
            P.op("sp", lambda e, i=i, src=src, nk=nk: e.dma_start(out=WS[i][:, 0:nk, :], in_=src),
                 writes=["WS%d" % i], dma_key="WS%d" % i)
            P.op("pool", lambda e, i=i, j=j, nk=nk: e.tensor_copy(out=WB[j][:, 0:nk, :], in_=WS[i][:, 0:nk, :]),
                 reads=["WS%d" % i], writes=["WB%d" % j])
            return j

        def rmsnorm(gcol0, tiles, final=False):
            for tt in tiles:
                t = tsl(tt)
                for c in range(NCH):
                    si = nxt("sq", 2)
                    P.op("act", lambda e, c=c, si=si, t=t: e.activation(out=SQ[si][:], in_=X[:, c, t], func=AF.Square),
                         reads=[xs(c, tt)], writes=["SQ%d" % si])
                    P.op("pe", lambda e, c=c, si=si: e.matmul(PS[6][:, :], lhsT=ONEB[:], rhs=SQ[si][:],
                                                              start=(c == 0), stop=(c == NCH - 1)),
                         reads=["SQ%d" % si, "ONEB"], writes=ps(6))
                P.op("act", lambda e: e.activation(out=RSTD[:], in_=PS[6][:, :], func=AF.Ln,
                                                   scale=1.0 / D, bias=EPS[:, 0:1]),
                     reads=ps(6) + ["EPS"], writes=["RSTD"])
                P.op("act", lambda e: e.activation(out=RSTD[:], in_=RSTD[:], func=AF.Exp, scale=-0.5),
                     reads=["RSTD"], writes=["RSTD"])
                for c in range(NCH):
                    if final:
                        P.op("dve", lambda e, c=c, t=t: e.scalar_tensor_tensor(
                            out=X[:, c, t], in0=X[:, c, t], scalar=PAR[:, gcol0 + c:gcol0 + c + 1], in1=RSTD[:],
                            op0=ALU.mult, op1=ALU.mult),
                            reads=[xs(c, tt), "RSTD", "PAR"], writes=[xs(c, tt)])
                    else:
                        P.op("dve", lambda e, c=c, t=t: e.scalar_tensor_tensor(
                            out=H[:, c, t], in0=X[:, c, t], scalar=PAR[:, gcol0 + c:gcol0 + c + 1], in1=RSTD[:],
                            op0=ALU.mult, op1=ALU.mult),
                            reads=[xs(c, tt), "RSTD", "PAR"], writes=[hs_(c, tt)])

        def ffn(l, which, gcol0):
            Wg, Wu, Wd = wg_d[which][l], wu_d[which][l], wd_d[which][l]
            for half in range(2):
                tiles = [2 * half, 2 * half + 1]
                rmsnorm(gcol0, tiles)
                for fc in range(NFC):
                    jg = wtile(Wg, 0, 8, fc * 128)
                    ju = wtile(Wu, 0, 8, fc * 128)
                    for tl, tt in enumerate(tiles):
                        t = tsl(tt)
                        k2 = nxt("zb", 2)
                        bg, bu = k2, 2 + k2
                        hreads = [hs_(c, tt) for c in range(NCH)]

                        def mm(e, j, b, t=t):
                            for c in range(NCH):
                                i = e.matmul(PS[b][:, :], lhsT=WB[j][:, c, :], rhs=H[:, c, t],
                                             start=(c == 0), stop=(c == NCH - 1))
                            return i
                        P.op("pe", lambda e, j=jg, b=bg, mm=mm: mm(e, j, b), reads=["WB%d" % jg] + hreads, writes=ps(bg))
                        P.op("pe", lambda e, j=ju, b=bu, mm=mm: mm(e, j, b), reads=["WB%d" % ju] + hreads, writes=ps(bu))
                        si = nxt("sg", 2)
                        P.op("act", lambda e, si=si, b=bg: e.activation(out=SG[si][:], in_=PS[b][:, :], func=AF.Silu),
                             reads=ps(bg), writes=["SG%d" % si])
                        aoff = (fc * 1024 + tl * TS) * 2
                        P.op("dve", lambda e, si=si, b=bu, aoff=aoff: e.tensor_tensor(
                            out=rb(aoff, TS), in0=SG[si][:], in1=PS[b][:, :], op=ALU.mult),
                            reads=["SG%d" % si] + ps(bu), writes=rg(aoff, TS * 2))
                for dc in range(NCH):
                    js = [wtile(Wd, 0, 8, dc * 128), wtile(Wd, 8, 8, dc * 128), wtile(Wd, 16, 6, dc * 128)]
                    for tl, tt in enumerate(tiles):
                        t = tsl(tt)
                        by = 4 + nxt("ev", 2)

                        for pc_ in range(3):
                            fcs = list(range(pc_ * 8, min(NFC, pc_ * 8 + 8)))

                            def mmd(e, js=js, by=by, tl=tl, fcs=fcs):
                                for fc in fcs:
                                    aoff = (fc * 1024 + tl * TS) * 2
                                    i = e.matmul(PS[by][:, :], lhsT=WB[js[fc // 8]][:, fc % 8, :], rhs=rb(aoff, TS),
                                                 start=(fc == 0), stop=(fc == NFC - 1))
                                return i
                            areads = []
                            for fc in fcs:
                                areads += rg((fc * 1024 + tl * TS) * 2, TS * 2)
                            P.op("pe", mmd, reads=["WB%d" % js[pc_]] + areads, writes=ps(by))
                        P.op("dve", lambda e, by=by, dc=dc, t=t: e.scalar_tensor_tensor(
                            out=X[:, dc, t], in0=PS[by][:, :], scalar=0.5, in1=X[:, dc, t], op0=ALU.mult, op1=ALU.add),
                            reads=ps(by) + [xs(dc, tt)], writes=[xs(dc, tt)])

        def zproj(j, tt, b):
            t = tsl(tt)

            def mm(e):
                for c in range(NCH):
                    i = e.matmul(PS[b][:, :], lhsT=WB[j][:, c, :], rhs=H[:, c, t], start=(c == 0), stop=(c == NCH - 1))
                return i
            P.op("pe", mm, reads=["WB%d" % j] + [hs_(c, tt) for c in range(NCH)], writes=ps(b))

        def yoff(yc, tt):
            return (yc * S + tt * TS) * 2

        def gconv(l, j):
            Win = w_in_d[l]
            pc = l * PL + 24 + j * 3
            T_o, DG_o, GX_o, GB_o = SB0, SB0 + 5 * 1024, SB0 + 6 * 1024, SB0 + 8 * 1024
            jb = wtile(Win, 0, 8, (12 + j) * 128)
            jc = wtile(Win, 0, 8, (14 + j) * 128)
            jx = wtile(Win, 0, 8, (16 + j) * 128)
            Tb = rb(T_o, S + 2)
            DG = rb(DG_o, 3 * 128).rearrange("p (k m) -> p k m", k=3)
            P.op("pool", lambda e: e.memset(Tb[:, 0:2], 0.0), writes=rg(T_o, 4))
            P.op("dve", lambda e: e.tensor_tensor(
                out=DG, in0=IDF[:].unsqueeze(1).to_broadcast([128, 3, 128]),
                in1=PAR[:, pc:pc + 3].unsqueeze(2).to_broadcast([128, 3, 128]), op=ALU.mult),
                reads=["IDF", "PAR"], writes=rg(DG_o, 768))
            for tt in range(NT):
                b0, b1, b2, b3 = nxt("zb", 4), nxt("zb", 4), nxt("zb", 4), nxt("zb", 4)
                gi = tt % 2
                zproj(jx, tt, b0)
                zproj(jc, tt, b1)
                zproj(jb, tt, b2)
                P.op("act", lambda e, b0=b0: e.copy(out=rf(GX_o, TS), in_=PS[b0][:, :]),
                     reads=ps(b0), writes=rg(GX_o, TS * 4))
                twr = rg(T_o + 4 + tt * TS * 2, TS * 2)
                P.op("dve", lambda e, b1=b1, tt=tt: e.tensor_tensor(
                    out=Tb[:, 2 + tt * TS: 2 + (tt + 1) * TS], in0=PS[b1][:, :], in1=rf(GX_o, TS), op=ALU.mult),
                    reads=ps(b1) + rg(GX_o, TS * 4), writes=twr)
                P.op("act", lambda e, b2=b2, gi=gi: e.copy(out=rf(GB_o + gi * 2048, TS), in_=PS[b2][:, :]),
                     reads=ps(b2), writes=rg(GB_o + gi * 2048, 2048))

                def cv(e, b3=b3, tt=tt):
                    for k in range(3):
                        i = e.matmul(PS[b3][:, :], lhsT=DG[:, k, :], rhs=Tb[:, tt * TS + k: tt * TS + k + TS],
                                     start=(k == 0), stop=(k == 2))
                    return i
                P.op("pe", cv, reads=rg(DG_o, 768) + rg(T_o + tt * TS * 2, TS * 2 + 4), writes=ps(b3))
                yo = yoff(4 + j, tt)
                P.op("dve", lambda e, b3=b3, gi=gi, yo=yo: e.tensor_tensor(
                    out=rb(yo, TS), in0=PS[b3][:, :], in1=rf(GB_o + gi * 2048, TS), op=ALU.mult),
                    reads=ps(b3) + rg(GB_o + gi * 2048, 2048), writes=rg(yo, TS * 2))

        U_o, DGC_o = SB0, SB0 + 5 * 1024
        V_o = [SB0 + 13 * 1024, SB0 + 21 * 1024]
        SQV_o, MEAN_o, RSC_o, MSQ_o, NRM_o, SGT_o = [SB0 + k * 1024 for k in (0, 2, 4, 6, 8, 29)]

        def cconv(l, j):
            Win = w_in_d[l]
            pc = l * PL + 30 + j * 31
            pb = l * PL + 92 + j
            jv = wtile(Win, 0, 8, (18 + j) * 128)
            jg = wtile(Win, 0, 8, (20 + j) * 128)
            Ub = rb(U_o, S + 30)
            DG = rb(DGC_o, 31 * 128).rearrange("p (k m) -> p k m", k=31)
            P.op("pool", lambda e: e.memset(Ub[:, 0:30], 0.0), writes=rg(U_o, 60))
            P.op("dve", lambda e: e.tensor_tensor(
                out=DG, in0=IDF[:].unsqueeze(1).to_broadcast([128, 31, 128]),
                in1=PAR[:, pc:pc + 31].unsqueeze(2).to_broadcast([128, 31, 128]), op=ALU.mult),
                reads=["IDF", "PAR"], writes=rg(DGC_o, 31 * 256))
            Vv = rf(V_o[j], S)
            for tt in range(NT):
                b0, b1, b2 = nxt("zb", 4), nxt("zb", 4), nxt("zb", 4)
                zproj(jg, tt, b0)
                zproj(jv, tt, b1)
                P.op("act", lambda e, b0=b0: e.activation(out=rf(SGT_o, TS), in_=PS[b0][:, :], func=AF.Sigmoid),
                     reads=ps(b0), writes=rg(SGT_o, TS * 4))
                P.op("dve", lambda e, b1=b1, tt=tt: e.tensor_tensor(
                    out=Ub[:, 30 + tt * TS: 30 + (tt + 1) * TS], in0=PS[b1][:, :], in1=rf(SGT_o, TS), op=ALU.mult),
                    reads=ps(b1) + rg(SGT_o, TS * 4), writes=rg(U_o + 60 + tt * TS * 2, TS * 2))

                def cv(e, b2=b2, tt=tt):
                    for k in range(31):
                        i = e.matmul(PS[b2][:, :], lhsT=DG[:, k, :], rhs=Ub[:, tt * TS + k: tt * TS + k + TS],
                                     start=(k == 0), stop=(k == 30))
                    return i
                P.op("pe", cv, reads=rg(DGC_o, 31 * 256) + rg(U_o + tt * TS * 2, TS * 2 + 60), writes=ps(b2))
                P.op("act", lambda e, b2=b2, tt=tt: e.activation(out=Vv[:, tsl(tt)], in_=PS[b2][:, :], func=AF.Identity,
                                                                 bias=PAR[:, pb:pb + 1], scale=1.0),
                     reads=ps(b2) + ["PAR"], writes=rg(V_o[j] + tt * TS * 4, TS * 4))

        def cconv_ln(l):
            pg = l * PL + 94
            pbb = l * PL + 96
            SQV2 = (SQV_o, SQV_o + 2048 + 8 * 1024)
            for tt in range(NT):
                t = tsl(tt)
                for j in range(2):
                    vsl = rf(V_o[j], S)[:, t]
                    vr = rg(V_o[j] + tt * TS * 4, TS * 4)
                    so = SQV2[(tt * 2 + j) % 2]
                    P.op("act", lambda e, vsl=vsl, so=so: e.activation(out=rf(so, TS), in_=vsl, func=AF.Square),
                         reads=vr, writes=rg(so, TS * 4))
                    P.op("pe", lambda e, j=j, so=so, tt=tt: e.matmul(PS[4 + tt][:, :], lhsT=ONEF[:], rhs=rf(so, TS),
                                                                     start=(j == 0), stop=(j == 1)),
                         reads=rg(so, TS * 4) + ["ONEF"], writes=ps(4 + tt))
                    P.op("pe", lambda e, j=j, vsl=vsl, tt=tt: e.matmul(PS[tt][:, :], lhsT=ONEF[:], rhs=vsl,
                                                                       start=(j == 0), stop=(j == 1)),
                         reads=vr + ["ONEF"], writes=ps(tt))
            for tt in range(NT):
                t = tsl(tt)
                MEAN, MSQ, RSC, NRM = rf(MEAN_o, TS), rf(MSQ_o, TS), rf(RSC_o, TS), rf(NRM_o, TS)
                mr, qr, rr, nr = rg(MEAN_o, TS * 4), rg(MSQ_o, TS * 4), rg(RSC_o, TS * 4), rg(NRM_o, TS * 4)
                P.op("dve", lambda e, tt=tt: e.tensor_scalar(out=MEAN, in0=PS[tt][:, :], scalar1=1.0 / 256, scalar2=None,
                                                             op0=ALU.mult), reads=ps(tt), writes=mr)
                P.op("dve", lambda e: e.tensor_tensor(out=MSQ, in0=MEAN, in1=MEAN, op=ALU.mult), reads=mr, writes=qr)
                P.op("dve", lambda e, tt=tt: e.scalar_tensor_tensor(out=RSC, in0=PS[4 + tt][:, :], scalar=1.0 / 256, in1=MSQ,
                                                                    op0=ALU.mult, op1=ALU.subtract),
                     reads=ps(4 + tt) + qr, writes=rr)
                P.op("act", lambda e: e.activation(out=RSC, in_=RSC, func=AF.Ln, scale=1.0, bias=EPS[:, 1:2]),
                     reads=rr + ["EPS"], writes=rr)
                P.op("act", lambda e: e.activation(out=RSC, in_=RSC, func=AF.Exp, scale=-0.5), reads=rr, writes=rr)
                for j in range(2):
                    vsl = rf(V_o[j], S)[:, t]
                    vr = rg(V_o[j] + tt * TS * 4, TS * 4)
                    P.op("dve", lambda e, vsl=vsl: e.tensor_tensor(out=NRM, in0=vsl, in1=MEAN, op=ALU.subtract),
                         reads=vr + mr, writes=nr)
                    P.op("dve", lambda e: e.tensor_tensor(out=NRM, in0=NRM, in1=RSC, op=ALU.mult),
                         reads=nr + rr, writes=nr)
                    yo = yoff(6 + j, tt)
                    P.op("act", lambda e, yo=yo, j=j: e.activation(
                        out=rb(yo, TS), in_=NRM, func=AF.Silu, bias=PAR[:, pbb + j:pbb + j + 1],
                        scale=PAR[:, pg + j:pg + j + 1]),
                        reads=nr + ["PAR"], writes=rg(yo, TS * 2))

        QT_o, KT_o, VT_o = SB0, SB0 + 4096, SB0 + 8192
        VOA_o, VOB_o = SB0 + 12 * 1024, SB0 + 16 * 1024
        ND_o = SB0 + 20 * 1024
        PT_o = SB0 + 36 * 1024
        NPT = 6
        DILS = (1, 4, 16)

        def tokset(view, o, i):
            dil = DILS[o]
            nb = 16 // dil
            r, b = i // nb, i % nb
            if dil == 1:
                return view[:, b * 128:(b + 1) * 128]
            return view.rearrange("p (m r) -> p r m", r=dil)[:, r, b * 128:(b + 1) * 128]

        def tokset_slots(base_o, o, i, esz):
            if o == 0:
                return rg(base_o + i * 128 * esz, 128 * esz)
            return rg(base_o, S * esz)

        def attention(l, hp, pre=None):
            Win = w_in_d[l]
            jq = wtile(Win, 0, 8, hp * 128)
            jk = wtile(Win, 0, 8, (4 + hp) * 128)
            jv = wtile(Win, 0, 8, (8 + hp) * 128)
            QT, KT, VT = rb(QT_o, S), rb(KT_o, S), rb(VT_o, S)
            VOA = rb(VOA_o, S).rearrange("p (b f) -> p b f", f=128)
            VOB = rb(VOB_o, S).rearrange("p (b f) -> p b f", f=128)
            P.op("pool", lambda e: e.memset(VOA[:, :, 64:128], 0.0), writes=rg(VOA_o, 4096))
            P.op("pool", lambda e: e.memset(VOB[:, :, 0:64], 0.0), writes=rg(VOB_o, 4096))
            for tt in range(NT):
                t = tsl(tt)
                for (jw, dst, dst_o, eng) in ((jq, QT, QT_o, "act"), (jk, KT, KT_o, "dve"), (jv, VT, VT_o, "act")):
                    b = nxt("zb", 4)
                    zproj(jw, tt, b)
                    if eng == "act":
                        P.op("act", lambda e, b=b, dst=dst, t=t: e.copy(out=dst[:, t], in_=PS[b][:, :]),
                             reads=ps(b), writes=rg(dst_o + tt * TS * 2, TS * 2))
                    else:
                        P.op("dve", lambda e, b=b, dst=dst, t=t: e.tensor_copy(out=dst[:, t], in_=PS[b][:, :]),
                             reads=ps(b), writes=rg(dst_o + tt * TS * 2, TS * 2))
            if pre is not None:
                pre()
            ND = rf(ND_o, 2 * S).rearrange("p (n s) -> p n s", n=2)
            SBK = ((0, 1, 2), (3, 4, 5))
            NDB = (6, 7)
            for o in range(3):
                dil = DILS[o]
                nb = 16 // dil
                for grp in range(2):
                    b = nxt("zb", 4)
                    psb = PS[b][:, :].bitcast(BF16)

                    def trs(e, o=o, grp=grp, psb=psb):
                        for k in range(8):
                            i = e.transpose(out=psb[:, k * 128:(k + 1) * 128], in_=tokset(VT, o, grp * 8 + k),
                                            identity=IDB[:])
                        return i
                    P.op("pe", trs, reads=rg(VT_o, S * 2) + ["IDB"], writes=ps(b))
                    pv3 = psb.rearrange("p (b f) -> p b f", f=128)
                    P.op("act", lambda e, grp=grp, pv3=pv3: e.copy(
                        out=VOA[:, grp * 8:(grp + 1) * 8, 0:64], in_=pv3[:, :, 0:64]),
                        reads=ps(b), writes=rg(VOA_o + grp * 2048, 2048))
                    P.op("act", lambda e, grp=grp, pv3=pv3: e.copy(
                        out=VOB[:, grp * 8:(grp + 1) * 8, 64:128], in_=pv3[:, :, 64:128]),
                        reads=ps(b), writes=rg(VOB_o + grp * 2048, 2048))

                def s_mm(i, o=o, nb=nb):
                    hasp = (i % nb) != 0
                    ncol = 256 if hasp else 128
                    banks = [SBK[a][i % 3] for a in range(2)]

                    def f(e, banks=banks, i=i, hasp=hasp, o=o, ncol=ncol):
                        for a in range(2):
                            rows = slice(a * 64, (a + 1) * 64)
                            e.matmul(PS[banks[a]][:, 0:128], lhsT=tokset(KT[rows, :], o, i),
                                     rhs=tokset(QT[rows, :], o, i), start=True, stop=False, skip_group_check=True)
                            if hasp:
                                e.matmul(PS[banks[a]][:, 128:256], lhsT=tokset(KT[rows, :], o, i - 1),
                                         rhs=tokset(QT[rows, :], o, i), start=False, stop=False, skip_group_check=True)
                        for a in range(2):
                            tid = 2 * hp + a + 4 - 2 * o
                            ins = e.matmul(PS[banks[a]][:, 0:ncol], lhsT=IDB[:], rhs=BIASB[:, tid, 0:ncol],
                                           start=False, stop=True, skip_group_check=True)
                        return ins
                    rd = tokset_slots(QT_o, o, i, 2) + tokset_slots(KT_o, o, i, 2)
                    if hasp:
                        rd = rd + tokset_slots(KT_o, o, i - 1, 2)
                    P.op("pe", f, reads=rd + ["IDB", "BIASB"], writes=ps(banks[0]) + ps(banks[1]))

                s_mm(0)
                s_mm(1)
                for i in range(16):
                    g2, k = i // 2, i % 2
                    hasp = (i % nb) != 0
                    ncol = 256 if hasp else 128
                    if i + 2 < 16:
                        s_mm(i + 2)
                    pts = []
                    for a in range(2):
                        bank = SBK[a][i % 3]
                        pi = nxt("pt", NPT)
                        po = PT_o + pi * 512
                        pts.append((pi, po))
                        P.op("act", lambda e, bank=bank, ncol=ncol, po=po: e.activation(
                            out=rb(po, ncol), in_=PS[bank][:, 0:ncol], func=AF.Exp, scale=0.125),
                            reads=ps(bank), writes=["PT%d" % pi])
                    ndb = NDB[g2 % 2]

                    def pv(e, ndb=ndb, pts=pts, k=k, i=i, hasp=hasp):
                        cs = slice(k * 128, (k + 1) * 128)
                        ds = slice(256 + k * 128, 256 + (k + 1) * 128)
                        for (dst, lhs) in ((cs, (VOA, VOB)), (ds, None)):
                            steps = []
                            for a in range(2):
                                po = pts[a][1]
                                if lhs is None:
                                    w_c = w_p = (ONEA0 if a == 0 else ONEB0)[:]
                                else:
                                    w_c, w_p = lhs[a][:, i, :], (lhs[a][:, i - 1, :] if hasp else None)
                                steps.append((w_c, rb(po, 128)))
                                if hasp:
                                    steps.append((w_p, rb(po + 256, 128)))
                            for n, (w_, r_) in enumerate(steps):
                                ins = e.matmul(PS[ndb][:, dst], lhsT=w_, rhs=r_, start=(n == 0), stop=(n == len(steps) - 1))
                        return ins
                    vrd = rg(VOA_o + i * 256, 256) + rg(VOB_o + i * 256, 256)
                    if hasp:
                        vrd = vrd + rg(VOA_o + (i - 1) * 256, 256) + rg(VOB_o + (i - 1) * 256, 256)
                    P.op("pe", pv, reads=["PT%d" % pts[0][0], "PT%d" % pts[1][0], "ONEA0", "ONEB0"] + vrd, writes=ps(ndb))
                    if k == 1:
                        if o == 0:
                            dstv = ND[:, :, g2 * 256:(g2 + 1) * 256]
                            srcv = PS[ndb][:, :].rearrange("p (n m) -> p n m", n=2)
                            slots = rg(ND_o + g2 * 1024, 1024) + rg(ND_o + 8192 + g2 * 1024, 1024)
                            P.op("dve", lambda e, dstv=dstv, srcv=srcv: e.tensor_copy(out=dstv, in_=srcv),
                                 reads=ps(ndb), writes=slots)
                        else:
                            if o == 1:
                                r_, b_ = (2 * g2) // 4, (2 * g2) % 4
                                dstv = ND.rearrange("p n (m r) -> p n r m", r=4)[:, :, r_, b_ * 128:b_ * 128 + 256]
                                srcv = PS[ndb][:, :].rearrange("p (n m) -> p n m", n=2)
                            else:
                                dstv = ND.rearrange("p n (m r) -> p n r m", r=16)[:, :, 2 * g2:2 * g2 + 2, :]
                                srcv = PS[ndb][:, :].rearrange("p (n k m) -> p n k m", n=2, k=2)
                            slots = rg(ND_o, 2 * S * 4)
                            P.op("dve", lambda e, dstv=dstv, srcv=srcv: e.tensor_tensor(
                                out=dstv, in0=srcv, in1=dstv, op=ALU.add),
                                reads=ps(ndb) + slots, writes=slots)
            def fin(hp=hp, ND=ND):
                for tt in range(NT):
                    t = tsl(tt)
                    NUMt, DENt = ND[:, 0, t], ND[:, 1, t]
                    dr, nr_ = rg(ND_o + 8192 + tt * TS * 4, TS * 4), rg(ND_o + tt * TS * 4, TS * 4)
                    P.op("dve", lambda e, DENt=DENt: e.reciprocal(out=DENt, in_=DENt), reads=dr, writes=dr)
                    yo = yoff(hp, tt)
                    P.op("dve", lambda e, yo=yo, NUMt=NUMt, DENt=DENt: e.tensor_tensor(
                        out=rb(yo, TS), in0=NUMt, in1=DENt, op=ALU.mult),
                        reads=dr + nr_, writes=rg(yo, TS * 2))
            return fin

        def mixer(l):
            rmsnorm(l * PL + 8, list(range(NT)))
            if len(mixparts) < 4:
                P.op("dve", lambda e: e.memset(rb(0, 8 * S), 0.0), writes=rg(0, 8 * S * 2))
            if "gconv" in mixparts:
                for j in range(2):
                    gconv(l, j)
            if "cconv" in mixparts:
                for j in range(2):
                    cconv(l, j)
                cconv_ln(l)
            if "attn" in mixparts:
                pend = None
                for hp in range(4):
                    pend = attention(l, hp, pre=pend)
                pend()
            Wout = w_out_d[l]
            for dc in range(NCH if "wout" in mixparts else 0):
                jw = wtile(Wout, 0, 8, dc * 128)
                for tt in range(NT):
                    t = tsl(tt)
                    b = nxt("zb", 4)

                    def mm(e, jw=jw, b=b, tt=tt):
                        for yc in range(NCH):
                            i = e.matmul(PS[b][:, :], lhsT=WB[jw][:, yc, :], rhs=rb(yoff(yc, tt), TS),
                                         start=(yc == 0), stop=(yc == NCH - 1))
                        return i
                    yr = []
                    for yc in range(NCH):
                        yr += rg(yoff(yc, tt), TS * 2)
                    P.op("pe", mm, reads=["WB%d" % jw] + yr, writes=ps(b))
                    P.op("dve", lambda e, b=b, dc=dc, t=t: e.tensor_tensor(
                        out=X[:, dc, t], in0=PS[b][:, :], in1=X[:, dc, t], op=ALU.add),
                        reads=ps(b) + [xs(dc, tt)], writes=[xs(dc, tt)])

        def load_x(s):
            for tt in range(NT):
                bi = tt % 2
                off = bi * 16384
                XSv = rf(off, 4 * D).rearrange("p (k d) -> p k d", k=4)
                src = x_d[s, tt * TS:(tt + 1) * TS, :].rearrange("(k p) d -> p k d", p=128)
                P.op("sp", lambda e, XSv=XSv, src=src: e.dma_start(out=XSv, in_=src),
                     writes=rg(off, 16384), dma_key="XS%d" % bi)
                for c in range(NCH):
                    b = nxt("zb", 4)

                    def trs(e, XSv=XSv, c=c, b=b):
                        for k in range(4):
                            i = e.transpose(out=PS[b][:, k * 128:(k + 1) * 128], in_=XSv[:, k, c * 128:(c + 1) * 128],
                                            identity=IDF[:])
                        return i
                    P.op("pe", trs, reads=rg(off, 16384) + ["IDF"], writes=ps(b))
                    if c % 2 == 0:
                        P.op("act", lambda e, b=b, c=c, tt=tt: e.copy(out=X[:, c, tsl(tt)], in_=PS[b][:, :]),
                             reads=ps(b), writes=[xs(c, tt)])
                    else:
                        P.op("dve", lambda e, b=b, c=c, tt=tt: e.tensor_copy(out=X[:, c, tsl(tt)], in_=PS[b][:, :]),
                             reads=ps(b), writes=[xs(c, tt)])

        finals = []

        def store_out(s):
            for tt in range(NT):
                bi = tt % 2
                off = bi * 16384
                OSv = rf(off, 4 * D).rearrange("p (k d) -> p k d", k=4)
                for k in range(4):
                    for hf in range(2):
                        b = nxt("zb", 4)

                        def trs(e, k=k, hf=hf, b=b, tt=tt):
                            for q in range(4):
                                c = hf * 4 + q
                                i = e.transpose(out=PS[b][:, q * 128:(q + 1) * 128],
                                                in_=X[:, c, tt * TS + k * 128: tt * TS + (k + 1) * 128], identity=IDF[:])
                            return i
                        P.op("pe", trs, reads=[xs(hf * 4 + q, tt) for q in range(4)] + ["IDF"], writes=ps(b))
                        wr = rg(off + k * 4096 + hf * 2048, 2048)
                        if (k + hf) % 2 == 0:
                            P.op("act", lambda e, OSv=OSv, k=k, hf=hf, b=b: e.copy(
                                out=OSv[:, k, hf * 512:(hf + 1) * 512], in_=PS[b][:, :]), reads=ps(b), writes=wr)
                        else:
                            P.op("dve", lambda e, OSv=OSv, k=k, hf=hf, b=b: e.tensor_copy(
                                out=OSv[:, k, hf * 512:(hf + 1) * 512], in_=PS[b][:, :]), reads=ps(b), writes=wr)
                dst = out_d[s, tt * TS:(tt + 1) * TS, :].rearrange("(k p) d -> p k d", p=128)
                finals.append(P.op("sp", lambda e, OSv=OSv, dst=dst: e.dma_start(out=dst, in_=OSv),
                                   reads=rg(off, 16384), dma_key="OS%d" % bi))

        for s in range(nseq):
            load_x(s)
            for l in layers:
                if "ffn1" in phases:
                    ffn(l, 0, l * PL + 0)
                if "mixer" in phases:
                    mixer(l)
                if "ffn2" in phases:
                    ffn(l, 1, l * PL + 16)
            if final_norm:
                rmsnorm(2 * PL, list(range(NT)), final=True)
            store_out(s)
        P.emit(nc, final_wait_ops=finals)
    return nc


def _params(inp):
    par = np.zeros((128, NPAR), np.float32)

    def cols(v):
        return np.ascontiguousarray(np.asarray(v, np.float32).reshape(-1, 128).T)
    for l in range(2):
        b = l * PL
        par[:, b + 0:b + 8] = cols(inp["norm_ffn1"][l])
        par[:, b + 8:b + 16] = cols(inp["norm_mix"][l])
        par[:, b + 16:b + 24] = cols(inp["norm_ffn2"][l])
        gw = np.asarray(inp["gconv_w"][l], np.float32)
        cw = np.asarray(inp["cconv_w"][l], np.float32)
        for j in range(2):
            par[:, b + 24 + j * 3: b + 24 + j * 3 + 3] = gw[:, j * 128:(j + 1) * 128].T
            par[:, b + 30 + j * 31: b + 30 + j * 31 + 31] = cw[:, j * 128:(j + 1) * 128].T
        par[:, b + 92:b + 94] = cols(inp["cconv_b"][l])
        par[:, b + 94:b + 96] = cols(inp["cln_g"][l])
        par[:, b + 96:b + 98] = cols(inp["cln_b"][l])
    par[:, 2 * PL:2 * PL + 8] = cols(inp["norm_final"])
    return par


def _masks():
    k = np.arange(128, dtype=np.float64)[:, None]
    q = np.arange(128, dtype=np.float64)[None, :]
    m = np.zeros((128, 12, 256), np.float64)
    NEG = -240000.0
    for tid in range(12):
        c = 2.0 ** (3 - tid)
        m[:, tid, 0:128] = np.where(q >= k, -8.0 * c * (q - k), NEG)
        m[:, tid, 128:256] = np.where(k >= q, -8.0 * c * (128 + q - k), NEG)
    return np.ascontiguousarray(m.reshape(128, 12 * 256).astype(np.float32))


_NC_CACHE = {}


def _get_nc(nseq, layers, final_norm):
    key = (nseq, tuple(layers), final_norm)
    if key not in _NC_CACHE:
        _NC_CACHE[key] = build_program(nseq, list(layers), final_norm)
    return _NC_CACHE[key]


def _run(x_shards, inp, layers, final_norm, n_cores):
    nseq = x_shards[0].shape[0]
    nc = _get_nc(nseq, layers, final_norm)
    common = {k: np.ascontiguousarray(np.asarray(inp[k], np.float32)) for k in
              ("w_in", "w_out", "ffn1_wg", "ffn1_wu", "ffn1_wd", "ffn2_wg", "ffn2_wu", "ffn2_wd")}
    common["params"] = _params(inp)
    common["masks"] = _masks()
    common["ident"] = np.eye(128, dtype=np.float32)
    in_maps = [dict(common, x=np.ascontiguousarray(xs_)) for xs_ in x_shards]
    res = run_bass_kernel_spmd(nc, in_maps, core_ids=list(range(n_cores)))
    return [np.asarray(r["out"]) for r in res.results]


FUSED = True


def kernel(**inputs):
    x = np.asarray(inputs["x"], np.float32)
    shards = [x[i * SEQ_PER_CORE:(i + 1) * SEQ_PER_CORE] for i in range(N_CORES)]
    if FUSED:
        outs = _run(shards, inputs, (0, 1), True, N_CORES)
    else:
        mid = _run(shards, inputs, (0,), False, N_CORES)
        outs = _run(mid, inputs, (1,), True, N_CORES)
    return np.concatenate(outs, axis=0).astype(np.float32)
```

```python
import contextlib
import numpy as np
import concourse.bass as bass
import concourse.mybir as mybir
from concourse.bass_utils import run_bass_kernel_spmd

F32 = mybir.dt.float32
BF16 = mybir.dt.bfloat16
ALU = mybir.AluOpType
AF = mybir.ActivationFunctionType

D = 1024
S = 2048
DFF = 2816
DIN = 2816
NCH = 8
NFC = 22
NT = 4
TS = 512
N_CORES = 8
SEQ_PER_CORE = 2
PL = 98
NPAR = 2 * PL + 8
RMS_EPS = 1e-6
LN_EPS = 1e-5

COMPUTE = ("pe", "act", "dve", "pool")


class _Op:
    __slots__ = ("eng", "fn", "deps", "is_dma", "key", "count", "signal", "sigidx", "idx")


class Prog:
    def __init__(self, same_engine_sync=True):
        self.ops = []
        self.last_writer = {}
        self.readers = {}
        self.dma_counts = {}
        self.same_engine_sync = same_engine_sync

    def op(self, eng, fn, reads=(), writes=(), dma_key=None):
        o = _Op()
        o.idx = len(self.ops)
        o.eng = eng
        o.fn = fn
        o.is_dma = dma_key is not None
        o.key = dma_key
        o.signal = False
        o.sigidx = 0
        o.count = 0
        deps = set()
        for s in reads:
            w = self.last_writer.get(s)
            if w is not None:
                deps.add(w)
        for s in writes:
            w = self.last_writer.get(s)
            if w is not None:
                deps.add(w)
            for r in self.readers.get(s, ()):
                deps.add(r)
        for s in reads:
            self.readers.setdefault(s, []).append(o.idx)
        for s in writes:
            self.readers[s] = []
            self.last_writer[s] = o.idx
        deps.discard(o.idx)
        if o.is_dma:
            c = self.dma_counts.get(dma_key, 0) + 1
            self.dma_counts[dma_key] = c
            o.count = c
        o.deps = deps
        self.ops.append(o)
        return o.idx

    def _skip(self, do, eng):
        if do.is_dma:
            return False
        if do.eng == "pe" and eng == "pe":
            return True
        if do.eng == eng and not self.same_engine_sync:
            return True
        return False

    def emit(self, nc, final_wait_ops=()):
        ops = self.ops
        for o in ops:
            for d in o.deps:
                do = ops[d]
                if not do.is_dma and not self._skip(do, o.eng):
                    do.signal = True
        cnt = {e: 0 for e in COMPUTE}
        for o in ops:
            if not o.is_dma and o.signal:
                cnt[o.eng] += 1
                o.sigidx = cnt[o.eng]
        with contextlib.ExitStack() as st:
            sems = {e: st.enter_context(nc.semaphore("s_" + e)) for e in COMPUTE}
            dsems = {k: st.enter_context(nc.semaphore("d_" + str(k))) for k in self.dma_counts}
            block = st.enter_context(nc.Block())
            streams = {}
            for o in ops:
                streams.setdefault(o.eng, []).append(o)

            def run_stream(ename, eobj):
                waited = {}
                for o in streams.get(ename, []):
                    need = {}
                    for d in o.deps:
                        do = ops[d]
                        if do.is_dma:
                            sem, val, k = dsems[do.key], 16 * do.count, ("d", do.key)
                        else:
                            if self._skip(do, ename):
                                continue
                            sem, val, k = sems[do.eng], do.sigidx, ("c", do.eng)
                        if need.get(k, (None, 0))[1] < val:
                            need[k] = (sem, val)
                    for k, (sem, val) in need.items():
                        if waited.get(k, 0) >= val:
                            continue
                        waited[k] = val
                        eobj.wait_ge(sem, val)
                    ins = o.fn(eobj)
                    if o.is_dma:
                        ins.then_inc(dsems[o.key], 16)
                    elif o.signal:
                        ins.then_inc(sems[o.eng], 1)
                if ename == "sp":
                    for d in final_wait_ops:
                        do = ops[d]
                        eobj.wait_ge(dsems[do.key], 16 * do.count)

            @block.sync
            def _(e):
                run_stream("sp", e)

            @block.tensor
            def _(e):
                run_stream("pe", e)

            @block.scalar
            def _(e):
                run_stream("act", e)

            @block.vector
            def _(e):
                run_stream("dve", e)

            @block.gpsimd
            def _(e):
                run_stream("pool", e)


def build_program(nseq, layers, final_norm, same_engine_sync=True, phases=("ffn1", "mixer", "ffn2"),
                  mixparts=("gconv", "cconv", "attn", "wout")):
    nc = bass.Bass("TRN2", target_bir_lowering=False)
    dram = lambda n, s, k: nc.dram_tensor(n, s, F32, kind=k).ap()
    x_d = dram("x", [nseq, S, D], "ExternalInput")
    out_d = dram("out", [nseq, S, D], "ExternalOutput")
    w_in_d = dram("w_in", [2, D, DIN], "ExternalInput")
    w_out_d = dram("w_out", [2, D, D], "ExternalInput")
    wg_d = [dram("ffn1_wg", [2, D, DFF], "ExternalInput"), dram("ffn2_wg", [2, D, DFF], "ExternalInput")]
    wu_d = [dram("ffn1_wu", [2, D, DFF], "ExternalInput"), dram("ffn2_wu", [2, D, DFF], "ExternalInput")]
    wd_d = [dram("ffn1_wd", [2, DFF, D], "ExternalInput"), dram("ffn2_wd", [2, DFF, D], "ExternalInput")]
    par_d = dram("params", [128, NPAR], "ExternalInput")
    mask_d = dram("masks", [128, 12 * 256], "ExternalInput")
    ident_d = dram("ident", [128, 128], "ExternalInput")

    P = Prog(same_engine_sync=same_engine_sync)
    NWS, NWB = 3, 6
    RBYTES = 71 * 1024
    SB0 = 32 * 1024

    with contextlib.ExitStack() as st:
        sb = lambda n, s, d: st.enter_context(nc.sbuf_tensor(n, s, d))
        X = sb("X", [128, NCH, S], F32)
        H = sb("H", [128, NCH, S], BF16)
        R = sb("R", [128, RBYTES // 2], BF16)
        WS = [sb("WS%d" % i, [128, 8, 128], F32) for i in range(NWS)]
        WB = [sb("WB%d" % i, [128, 8, 128], BF16) for i in range(NWB)]
        SQ = [sb("SQ%d" % i, [128, TS], BF16) for i in range(2)]
        SG = [sb("SG%d" % i, [128, TS], F32) for i in range(2)]
        RSTD = sb("RSTD", [128, TS], F32)
        PAR = sb("PAR", [128, NPAR], F32)
        BIASB = sb("BIASB", [128, 12, 256], BF16)
        ONEA0 = sb("ONEA0", [128, 128], BF16)
        ONEB0 = sb("ONEB0", [128, 128], BF16)
        IDF = sb("IDF", [128, 128], F32)
        IDB = sb("IDB", [128, 128], BF16)
        ONEB = sb("ONEB", [128, 128], BF16)
        ONEF = sb("ONEF", [128, 128], F32)
        EPS = sb("EPS", [128, 2], F32)
        PS = [st.enter_context(nc.psum_tensor("ps%d" % i, [128, TS], F32)) for i in range(8)]

        def rg(off, nbytes):
            return ["R%d" % g for g in range(off // 1024, (off + nbytes - 1) // 1024 + 1)]

        def rb(off, n):
            return R[:, off // 2: off // 2 + n]

        def rf(off, n):
            return R[:, off // 2: off // 2 + 2 * n].bitcast(F32)

        def ps(b):
            return ["P%d" % b]

        def xs(c, tt):
            return "X_%d_%d" % (c, tt)

        def hs_(c, tt):
            return "H_%d_%d" % (c, tt)

        def tsl(tt):
            return slice(tt * TS, (tt + 1) * TS)

        cnt = {"ws": 0, "wb": 0, "sq": 0, "sg": 0, "zb": 0, "e": 0, "pt": 0, "ev": 0}

        def nxt(k, n):
            v = cnt[k] % n
            cnt[k] += 1
            return v

        P.op("sp", lambda e: e.dma_start(out=PAR[:], in_=par_d), writes=["PAR"], dma_key="cPAR")
        P.op("sp", lambda e: e.dma_start(out=IDF[:], in_=ident_d), writes=["IDF"], dma_key="cIDF")
        P.op("pool", lambda e: e.tensor_copy(out=IDB[:], in_=IDF[:]), reads=["IDF"], writes=["IDB"])
        P.op("pool", lambda e: e.memset(ONEB[:], 1.0), writes=["ONEB"])
        P.op("pool", lambda e: e.memset(ONEF[:], 1.0), writes=["ONEF"])
        P.op("pool", lambda e: e.memset(ONEA0[:], 0.0), writes=["ONEA0"])
        P.op("pool", lambda e: e.memset(ONEA0[:, 0:64], 1.0), writes=["ONEA0"])
        P.op("pool", lambda e: e.memset(ONEB0[:], 0.0), writes=["ONEB0"])
        P.op("pool", lambda e: e.memset(ONEB0[:, 64:128], 1.0), writes=["ONEB0"])
        P.op("pool", lambda e: e.memset(EPS[:, 0:1], RMS_EPS), writes=["EPS"])
        P.op("pool", lambda e: e.memset(EPS[:, 1:2], LN_EPS), writes=["EPS"])
        for q in range(3):
            i = nxt("ws", NWS)
            P.op("sp", lambda e, q=q, i=i: e.dma_start(
                out=WS[i][:].rearrange("p a b -> p (a b)"), in_=mask_d[:, q * 1024:(q + 1) * 1024]),
                writes=["WS%d" % i], dma_key="WS%d" % i)
            P.op("pool", lambda e, q=q, i=i: e.tensor_copy(
                out=BIASB[:, q * 4:(q + 1) * 4, :],
                in_=WS[i][:].rearrange("p a b -> p (a b)").rearrange("p (t c) -> p t c", t=4)),
                reads=["WS%d" % i], writes=["BIASB"])

        def wtile(W2d, k0, nk, col0):
            i = nxt("ws", NWS)
            j = nxt("wb", NWB)
            src = W2d[k0 * 128:(k0 + nk) * 128, col0:col0 + 128].rearrange("(c p) f -> p c f", p=128)
            P.op("sp", lambda e, i=i, src=src, nk=nk: e.dma_start(out=WS[i][:, 0:nk, :], in_=src),
                 writes=["WS%d" % i], dma_key="WS%d" % i)
            P.op("pool", lambda e, i=i, j=j, nk=nk: e.tensor_copy(out=WB[j][:, 0:nk, :], in_=WS[i][:, 0:nk, :]),
                 reads=["WS%d" % i], writes=["WB%d" % j])
            return j

        def rmsnorm(gcol0, tiles, final=False):
            for tt in tiles:
                t = tsl(tt)
                for c in range(NCH):
                    si = nxt("sq", 2)
                    P.op("act", lambda e, c=c, si=si, t=t: e.activation(out=SQ[si][:], in_=X[:, c, t], func=AF.Square),
                         reads=[xs(c, tt)], writes=["SQ%d" % si])
                    P.op("pe", lambda e, c=c, si=si: e.matmul(PS[6][:, :], lhsT=ONEB[:], rhs=SQ[si][:],
                                                              start=(c == 0), stop=(c == NCH - 1)),
                         reads=["SQ%d" % si, "ONEB"], writes=ps(6))
                P.op("act", lambda e: e.activation(out=RSTD[:], in_=PS[6][:, :], func=AF.Ln,
                                                   scale=1.0 / D, bias=EPS[:, 0:1]),
                     reads=ps(6) + ["EPS"], writes=["RSTD"])
                P.op("act", lambda e: e.activation(out=RSTD[:], in_=RSTD[:], func=AF.Exp, scale=-0.5),
                     reads=["RSTD"], writes=["RSTD"])
                for c in range(NCH):
                    if final:
                        P.op("dve", lambda e, c=c, t=t: e.scalar_tensor_tensor(
                            out=X[:, c, t], in0=X[:, c, t], scalar=PAR[:, gcol0 + c:gcol0 + c + 1], in1=RSTD[:],
                            op0=ALU.mult, op1=ALU.mult),
                            reads=[xs(c, tt), "RSTD", "PAR"], writes=[xs(c, tt)])
                    else:
                        P.op("dve", lambda e, c=c, t=t: e.scalar_tensor_tensor(
                            out=H[:, c, t], in0=X[:, c, t], scalar=PAR[:, gcol0 + c:gcol0 + c + 1], in1=RSTD[:],
                            op0=ALU.mult, op1=ALU.mult),
                            reads=[xs(c, tt), "RSTD", "PAR"], writes=[hs_(c, tt)])

        def ffn(l, which, gcol0):
            Wg, Wu, Wd = wg_d[which][l], wu_d[which][l], wd_d[which][l]
            for half in range(2):
                tiles = [2 * half, 2 * half + 1]
                rmsnorm(gcol0, tiles)
                for fc in range(NFC):
                    jg = wtile(Wg, 0, 8, fc * 128)
                    ju = wtile(Wu, 0, 8, fc * 128)
                    for tl, tt in enumerate(tiles):
                        t = tsl(tt)
                        k2 = nxt("zb", 2)
                        bg, bu = k2, 2 + k2
                        hreads = [hs_(c, tt) for c in range(NCH)]

                        def mm(e, j, b, t=t):
                            for c in range(NCH):
                                i = e.matmul(PS[b][:, :], lhsT=WB[j][:, c, :], rhs=H[:, c, t],
                                             start=(c == 0), stop=(c == NCH - 1))
                            return i
                        P.op("pe", lambda e, j=jg, b=bg, mm=mm: mm(e, j, b), reads=["WB%d" % jg] + hreads, writes=ps(bg))
                        P.op("pe", lambda e, j=ju, b=bu, mm=mm: mm(e, j, b), reads=["WB%d" % ju] + hreads, writes=ps(bu))
                        si = nxt("sg", 2)
                        P.op("act", lambda e, si=si, b=bg: e.activation(out=SG[si][:], in_=PS[b][:, :], func=AF.Silu),
                             reads=ps(bg), writes=["SG%d" % si])
                        aoff = (fc * 1024 + tl * TS) * 2
                        P.op("dve", lambda e, si=si, b=bu, aoff=aoff: e.tensor_tensor(
                            out=rb(aoff, TS), in0=SG[si][:], in1=PS[b][:, :], op=ALU.mult),
                            reads=["SG%d" % si] + ps(bu), writes=rg(aoff, TS * 2))
                for dc in range(NCH):
                    js = [wtile(Wd, 0, 8, dc * 128), wtile(Wd, 8, 8, dc * 128), wtile(Wd, 16, 6, dc * 128)]
                    for tl, tt in enumerate(tiles):
                        t = tsl(tt)
                        by = 4 + nxt("ev", 2)

                        for pc_ in range(3):
                            fcs = list(range(pc_ * 8, min(NFC, pc_ * 8 + 8)))

                            def mmd(e, js=js, by=by, tl=tl, fcs=fcs):
                                for fc in fcs:
                                    aoff = (fc * 1024 + tl * TS) * 2
                                    i = e.matmul(PS[by][:, :], lhsT=WB[js[fc // 8]][:, fc % 8, :], rhs=rb(aoff, TS),
                                                 start=(fc == 0), stop=(fc == NFC - 1))
                                return i
                            areads = []
                            for fc in fcs:
                                areads += rg((fc * 1024 + tl * TS) * 2, TS * 2)
                            P.op("pe", mmd, reads=["WB%d" % js[pc_]] + areads, writes=ps(by))
                        P.op("dve", lambda e, by=by, dc=dc, t=t: e.scalar_tensor_tensor(
                            out=X[:, dc, t], in0=PS[by][:, :], scalar=0.5, in1=X[:, dc, t], op0=ALU.mult, op1=ALU.add),
                            reads=ps(by) + [xs(dc, tt)], writes=[xs(dc, tt)])

        def zproj(j, tt, b):
            t = tsl(tt)

            def mm(e):
                for c in range(NCH):
                    i = e.matmul(PS[b][:, :], lhsT=WB[j][:, c, :], rhs=H[:, c, t], start=(c == 0), stop=(c == NCH - 1))
                return i
            P.op("pe", mm, reads=["WB%d" % j] + [hs_(c, tt) for c in range(NCH)], writes=ps(b))

        def yoff(yc, tt):
            return (yc * S + tt * TS) * 2

        def gconv(l, j):
            Win = w_in_d[l]
            pc = l * PL + 24 + j * 3
            T_o, DG_o, GX_o, GB_o = SB0, SB0 + 5 * 1024, SB0 + 6 * 1024, SB0 + 8 * 1024
            jb = wtile(Win, 0, 8, (12 + j) * 128)
            jc = wtile(Win, 0, 8, (14 + j) * 128)
            jx = wtile(Win, 0, 8, (16 + j) * 128)
            Tb = rb(T_o, S + 2)
            DG = rb(DG_o, 3 * 128).rearrange("p (k m) -> p k m", k=3)
            P.op("pool", lambda e: e.memset(Tb[:, 0:2], 0.0), writes=rg(T_o, 4))
            P.op("dve", lambda e: e.tensor_tensor(
                out=DG, in0=IDF[:].unsqueeze(1).to_broadcast([128, 3, 128]),
                in1=PAR[:, pc:pc + 3].unsqueeze(2).to_broadcast([128, 3, 128]), op=ALU.mult),
                reads=["IDF", "PAR"], writes=rg(DG_o, 768))
            for tt in range(NT):
                b0, b1, b2, b3 = nxt("zb", 4), nxt("zb", 4), nxt("zb", 4), nxt("zb", 4)
                gi = tt % 2
                zproj(jx, tt, b0)
                zproj(jc, tt, b1)
                zproj(jb, tt, b2)
                P.op("act", lambda e, b0=b0: e.copy(out=rf(GX_o, TS), in_=PS[b0][:, :]),
                     reads=ps(b0), writes=rg(GX_o, TS * 4))
                twr = rg(T_o + 4 + tt * TS * 2, TS * 2)
                P.op("dve", lambda e, b1=b1, tt=tt: e.tensor_tensor(
                    out=Tb[:, 2 + tt * TS: 2 + (tt + 1) * TS], in0=PS[b1][:, :], in1=rf(GX_o, TS), op=ALU.mult),
                    reads=ps(b1) + rg(GX_o, TS * 4), writes=twr)
                P.op("act", lambda e, b2=b2, gi=gi: e.copy(out=rf(GB_o + gi * 2048, TS), in_=PS[b2][:, :]),
                     reads=ps(b2), writes=rg(GB_o + gi * 2048, 2048))

                def cv(e, b3=b3, tt=tt):
                    for k in range(3):
                        i = e.matmul(PS[b3][:, :], lhsT=DG[:, k, :], rhs=Tb[:, tt * TS + k: tt * TS + k + TS],
                                     start=(k == 0), stop=(k == 2))
                    return i
                P.op("pe", cv, reads=rg(DG_o, 768) + rg(T_o + tt * TS * 2, TS * 2 + 4), writes=ps(b3))
                yo = yoff(4 + j, tt)
                P.op("dve", lambda e, b3=b3, gi=gi, yo=yo: e.tensor_tensor(
                    out=rb(yo, TS), in0=PS[b3][:, :], in1=rf(GB_o + gi * 2048, TS), op=ALU.mult),
                    reads=ps(b3) + rg(GB_o + gi * 2048, 2048), writes=rg(yo, TS * 2))

        U_o, DGC_o = SB0, SB0 + 5 * 1024
        V_o = [SB0 + 13 * 1024, SB0 + 21 * 1024]
        SQV_o, MEAN_o, RSC_o, MSQ_o, NRM_o, SGT_o = [SB0 + k * 1024 for k in (0, 2, 4, 6, 8, 29)]

        def cconv(l, j):
            Win = w_in_d[l]
            pc = l * PL + 30 + j * 31
            pb = l * PL + 92 + j
            jv = wtile(Win, 0, 8, (18 + j) * 128)
            jg = wtile(Win, 0, 8, (20 + j) * 128)
            Ub = rb(U_o, S + 30)
            DG = rb(DGC_o, 31 * 128).rearrange("p (k m) -> p k m", k=31)
            P.op("pool", lambda e: e.memset(Ub[:, 0:30], 0.0), writes=rg(U_o, 60))
            P.op("dve", lambda e: e.tensor_tensor(
                out=DG, in0=IDF[:].unsqueeze(1).to_broadcast([128, 31, 128]),
                in1=PAR[:, pc:pc + 31].unsqueeze(2).to_broadcast([128, 31, 128]), op=ALU.mult),
                reads=["IDF", "PAR"], writes=rg(DGC_o, 31 * 256))
            Vv = rf(V_o[j], S)
            for tt in range(NT):
                b0, b1, b2 = nxt("zb", 4), nxt("zb", 4), nxt("zb", 4)
                zproj(jg, tt, b0)
                zproj(jv, tt, b1)
                P.op("act", lambda e, b0=b0: e.activation(out=rf(SGT_o, TS), in_=PS[b0][:, :], func=AF.Sigmoid),
                     reads=ps(b0), writes=rg(SGT_o, TS * 4))
                P.op("dve", lambda e, b1=b1, tt=tt: e.tensor_tensor(
                    out=Ub[:, 30 + tt * TS: 30 + (tt + 1) * TS], in0=PS[b1][:, :], in1=rf(SGT_o, TS), op=ALU.mult),
                    reads=ps(b1) + rg(SGT_o, TS * 4), writes=rg(U_o + 60 + tt * TS * 2, TS * 2))

                def cv(e, b2=b2, tt=tt):
                    for k in range(31):
                        i = e.matmul(PS[b2][:, :], lhsT=DG[:, k, :], rhs=Ub[:, tt * TS + k: tt * TS + k + TS],
                                     start=(k == 0), stop=(k == 30))
                    return i
                P.op("pe", cv, reads=rg(DGC_o, 31 * 256) + rg(U_o + tt * TS * 2, TS * 2 + 60), writes=ps(b2))
                P.op("act", lambda e, b2=b2, tt=tt: e.activation(out=Vv[:, tsl(tt)], in_=PS[b2][:, :], func=AF.Identity,
                                                                 bias=PAR[:, pb:pb + 1], scale=1.0),
                     reads=ps(b2) + ["PAR"], writes=rg(V_o[j] + tt * TS * 4, TS * 4))

        def cconv_ln(l):
            pg = l * PL + 94
            pbb = l * PL + 96
            SQV2 = (SQV_o, SQV_o + 2048 + 8 * 1024)
            for tt in range(NT):
                t = tsl(tt)
                for j in range(2):
                    vsl = rf(V_o[j], S)[:, t]
                    vr = rg(V_o[j] + tt * TS * 4, TS * 4)
                    so = SQV2[(tt * 2 + j) % 2]
                    P.op("act", lambda e, vsl=vsl, so=so: e.activation(out=rf(so, TS), in_=vsl, func=AF.Square),
                         reads=vr, writes=rg(so, TS * 4))
                    P.op("pe", lambda e, j=j, so=so, tt=tt: e.matmul(PS[4 + tt][:, :], lhsT=ONEF[:], rhs=rf(so, TS),
                                                                     start=(j == 0), stop=(j == 1)),
                         reads=rg(so, TS * 4) + ["ONEF"], writes=ps(4 + tt))
                    P.op("pe", lambda e, j=j, vsl=vsl, tt=tt: e.matmul(PS[tt][:, :], lhsT=ONEF[:], rhs=vsl,
                                                                       start=(j == 0), stop=(j == 1)),
                         reads=vr + ["ONEF"], writes=ps(tt))
            for tt in range(NT):
                t = tsl(tt)
                MEAN, MSQ, RSC, NRM = rf(MEAN_o, TS), rf(MSQ_o, TS), rf(RSC_o, TS), rf(NRM_o, TS)
                mr, qr, rr, nr = rg(MEAN_o, TS * 4), rg(MSQ_o, TS * 4), rg(RSC_o, TS * 4), rg(NRM_o, TS * 4)
                P.op("dve", lambda e, tt=tt: e.tensor_scalar(out=MEAN, in0=PS[tt][:, :], scalar1=1.0 / 256, scalar2=None,
                                                             op0=ALU.mult), reads=ps(tt), writes=mr)
                P.op("dve", lambda e: e.tensor_tensor(out=MSQ, in0=MEAN, in1=MEAN, op=ALU.mult), reads=mr, writes=qr)
                P.op("dve", lambda e, tt=tt: e.scalar_tensor_tensor(out=RSC, in0=PS[4 + tt][:, :], scalar=1.0 / 256, in1=MSQ,
                                                                    op0=ALU.mult, op1=ALU.subtract),
                     reads=ps(4 + tt) + qr, writes=rr)
                P.op("act", lambda e: e.activation(out=RSC, in_=RSC, func=AF.Ln, scale=1.0, bias=EPS[:, 1:2]),
                     reads=rr + ["EPS"], writes=rr)
                P.op("act", lambda e: e.activation(out=RSC, in_=RSC, func=AF.Exp, scale=-0.5), reads=rr, writes=rr)
                for j in range(2):
                    vsl = rf(V_o[j], S)[:, t]
                    vr = rg(V_o[j] + tt * TS * 4, TS * 4)
                    P.op("dve", lambda e, vsl=vsl: e.tensor_tensor(out=NRM, in0=vsl, in1=MEAN, op=ALU.subtract),
                         reads=vr + mr, writes=nr)
                    P.op("dve", lambda e: e.tensor_tensor(out=NRM, in0=NRM, in1=RSC, op=ALU.mult),
                         reads=nr + rr, writes=nr)
                    yo = yoff(6 + j, tt)
                    P.op("act", lambda e, yo=yo, j=j: e.activation(
                        out=rb(yo, TS), in_=NRM, func=AF.Silu, bias=PAR[:, pbb + j:pbb + j + 1],
                        scale=PAR[:, pg + j:pg + j + 1]),
                        reads=nr + ["PAR"], writes=rg(yo, TS * 2))

        QT_o, KT_o, VT_o = SB0, SB0 + 4096, SB0 + 8192
        VOA_o, VOB_o = SB0 + 12 * 1024, SB0 + 16 * 1024
        ND_o = SB0 + 20 * 1024
        PT_o = SB0 + 36 * 1024
        NPT = 6
        DILS = (1, 4, 16)

        def tokset(view, o, i):
            dil = DILS[o]
            nb = 16 // dil
            r, b = i // nb, i % nb
            if dil == 1:
                return view[:, b * 128:(b + 1) * 128]
            return view.rearrange("p (m r) -> p r m", r=dil)[:, r, b * 128:(b + 1) * 128]

        def tokset_slots(base_o, o, i, esz):
            if o == 0:
                return rg(base_o + i * 128 * esz, 128 * esz)
            return rg(base_o, S * esz)

        def attention(l, hp):
            Win = w_in_d[l]
            jq = wtile(Win, 0, 8, hp * 128)
            jk = wtile(Win, 0, 8, (4 + hp) * 128)
            jv = wtile(Win, 0, 8, (8 + hp) * 128)
            QT, KT, VT = rb(QT_o, S), rb(KT_o, S), rb(VT_o, S)
            VOA = rb(VOA_o, S).rearrange("p (b f) -> p b f", f=128)
            VOB = rb(VOB_o, S).rearrange("p (b f) -> p b f", f=128)
            P.op("pool", lambda e: e.memset(VOA[:, :, 64:128], 0.0), writes=rg(VOA_o, 4096))
            P.op("pool", lambda e: e.memset(VOB[:, :, 0:64], 0.0), writes=rg(VOB_o, 4096))
            for tt in range(NT):
                t = tsl(tt)
                for (jw, dst, dst_o, eng) in ((jq, QT, QT_o, "act"), (jk, KT, KT_o, "dve"), (jv, VT, VT_o, "act")):
                    b = nxt("zb", 4)
                    zproj(jw, tt, b)
                    if eng == "act":
                        P.op("act", lambda e, b=b, dst=dst, t=t: e.copy(out=dst[:, t], in_=PS[b][:, :]),
                             reads=ps(b), writes=rg(dst_o + tt * TS * 2, TS * 2))
                    else:
                        P.op("dve", lambda e, b=b, dst=dst, t=t: e.tensor_copy(out=dst[:, t], in_=PS[b][:, :]),
                             reads=ps(b), writes=rg(dst_o + tt * TS * 2, TS * 2))
            ND = rf(ND_o, 2 * S).rearrange("p (n s) -> p n s", n=2)
            SBK = ((0, 1, 2), (3, 4, 5))
            NDB = (6, 7)
            for o in range(3):
                dil = DILS[o]
                nb = 16 // dil
                for grp in range(2):
                    b = nxt("zb", 4)
                    psb = PS[b][:, :].bitcast(BF16)

                    def trs(e, o=o, grp=grp, psb=psb):
                        for k in range(8):
                            i = e.transpose(out=psb[:, k * 128:(k + 1) * 128], in_=tokset(VT, o, grp * 8 + k),
                                            identity=IDB[:])
                        return i
                    P.op("pe", trs, reads=rg(VT_o, S * 2) + ["IDB"], writes=ps(b))
                    pv3 = psb.rearrange("p (b f) -> p b f", f=128)
                    P.op("act", lambda e, grp=grp, pv3=pv3: e.copy(
                        out=VOA[:, grp * 8:(grp + 1) * 8, 0:64], in_=pv3[:, :, 0:64]),
                        reads=ps(b), writes=rg(VOA_o + grp * 2048, 2048))
                    P.op("act", lambda e, grp=grp, pv3=pv3: e.copy(
                        out=VOB[:, grp * 8:(grp + 1) * 8, 64:128], in_=pv3[:, :, 64:128]),
                        reads=ps(b), writes=rg(VOB_o + grp * 2048, 2048))

                def s_mm(i, o=o, nb=nb):
                    hasp = (i % nb) != 0
                    ncol = 256 if hasp else 128
                    banks = [SBK[a][i % 3] for a in range(2)]

                    def f(e, banks=banks, i=i, hasp=hasp, o=o, ncol=ncol):
                        for a in range(2):
                            rows = slice(a * 64, (a + 1) * 64)
                            e.matmul(PS[banks[a]][:, 0:128], lhsT=tokset(KT[rows, :], o, i),
                                     rhs=tokset(QT[rows, :], o, i), start=True, stop=False, skip_group_check=True)
                            if hasp:
                                e.matmul(PS[banks[a]][:, 128:256], lhsT=tokset(KT[rows, :], o, i - 1),
                                         rhs=tokset(QT[rows, :], o, i), start=False, stop=False, skip_group_check=True)
                        for a in range(2):
                            tid = 2 * hp + a + 4 - 2 * o
                            ins = e.matmul(PS[banks[a]][:, 0:ncol], lhsT=IDB[:], rhs=BIASB[:, tid, 0:ncol],
                                           start=False, stop=True, skip_group_check=True)
                        return ins
                    rd = tokset_slots(QT_o, o, i, 2) + tokset_slots(KT_o, o, i, 2)
                    if hasp:
                        rd = rd + tokset_slots(KT_o, o, i - 1, 2)
                    P.op("pe", f, reads=rd + ["IDB", "BIASB"], writes=ps(banks[0]) + ps(banks[1]))

                s_mm(0)
                s_mm(1)
                for i in range(16):
                    g2, k = i // 2, i % 2
                    hasp = (i % nb) != 0
                    ncol = 256 if hasp else 128
                    if i + 2 < 16:
                        s_mm(i + 2)
                    pts = []
                    for a in range(2):
                        bank = SBK[a][i % 3]
                        pi = nxt("pt", NPT)
                        po = PT_o + pi * 512
                        pts.append((pi, po))
                        P.op("act", lambda e, bank=bank, ncol=ncol, po=po: e.activation(
                            out=rb(po, ncol), in_=PS[bank][:, 0:ncol], func=AF.Exp, scale=0.125),
                            reads=ps(bank), writes=["PT%d" % pi])
                    ndb = NDB[g2 % 2]

                    def pv(e, ndb=ndb, pts=pts, k=k, i=i, hasp=hasp):
                        cs = slice(k * 128, (k + 1) * 128)
                        ds = slice(256 + k * 128, 256 + (k + 1) * 128)
                        for (dst, lhs) in ((cs, (VOA, VOB)), (ds, None)):
                            steps = []
                            for a in range(2):
                                po = pts[a][1]
                                if lhs is None:
                                    w_c = w_p = (ONEA0 if a == 0 else ONEB0)[:]
                                else:
                                    w_c, w_p = lhs[a][:, i, :], (lhs[a][:, i - 1, :] if hasp else None)
                                steps.append((w_c, rb(po, 128)))
                                if hasp:
                                    steps.append((w_p, rb(po + 256, 128)))
                            for n, (w_, r_) in enumerate(steps):
                                ins = e.matmul(PS[ndb][:, dst], lhsT=w_, rhs=r_, start=(n == 0), stop=(n == len(steps) - 1))
                        return ins
                    vrd = rg(VOA_o + i * 256, 256) + rg(VOB_o + i * 256, 256)
                    if hasp:
                        vrd = vrd + rg(VOA_o + (i - 1) * 256, 256) + rg(VOB_o + (i - 1) * 256, 256)
                    P.op("pe", pv, reads=["PT%d" % pts[0][0], "PT%d" % pts[1][0], "ONEA0", "ONEB0"] + vrd, writes=ps(ndb))
                    if k == 1:
                        if o == 0:
                            dstv = ND[:, :, g2 * 256:(g2 + 1) * 256]
                            srcv = PS[ndb][:, :].rearrange("p (n m) -> p n m", n=2)
                            slots = rg(ND_o + g2 * 1024, 1024) + rg(ND_o + 8192 + g2 * 1024, 1024)
                            P.op("dve", lambda e, dstv=dstv, srcv=srcv: e.tensor_copy(out=dstv, in_=srcv),
                                 reads=ps(ndb), writes=slots)
                        else:
                            if o == 1:
                                r_, b_ = (2 * g2) // 4, (2 * g2) % 4
                                dstv = ND.rearrange("p n (m r) -> p n r m", r=4)[:, :, r_, b_ * 128:b_ * 128 + 256]
                                srcv = PS[ndb][:, :].rearrange("p (n m) -> p n m", n=2)
                            else:
                                dstv = ND.rearrange("p n (m r) -> p n r m", r=16)[:, :, 2 * g2:2 * g2 + 2, :]
                                srcv = PS[ndb][:, :].rearrange("p (n k m) -> p n k m", n=2, k=2)
                            slots = rg(ND_o, 2 * S * 4)
                            P.op("dve", lambda e, dstv=dstv, srcv=srcv: e.tensor_tensor(
                                out=dstv, in0=srcv, in1=dstv, op=ALU.add),
                                reads=ps(ndb) + slots, writes=slots)
            for tt in range(NT):
                t = tsl(tt)
                NUMt, DENt = ND[:, 0, t], ND[:, 1, t]
                dr, nr_ = rg(ND_o + 8192 + tt * TS * 4, TS * 4), rg(ND_o + tt * TS * 4, TS * 4)
                if hp == 3:
                    P.op("act", lambda e, DENt=DENt: e.activation(out=DENt, in_=DENt, func=AF.Ln), reads=dr, writes=dr)
                    P.op("act", lambda e, DENt=DENt: e.activation(out=DENt, in_=DENt, func=AF.Exp, scale=-1.0),
                         reads=dr, writes=dr)
                else:
                    P.op("dve", lambda e, DENt=DENt: e.reciprocal(out=DENt, in_=DENt), reads=dr, writes=dr)
                yo = yoff(hp, tt)
                P.op("dve", lambda e, yo=yo, NUMt=NUMt, DENt=DENt: e.tensor_tensor(
                    out=rb(yo, TS), in0=NUMt, in1=DENt, op=ALU.mult),
                    reads=dr + nr_, writes=rg(yo, TS * 2))

        def mixer(l):
            rmsnorm(l * PL + 8, list(range(NT)))
            if len(mixparts) < 4:
                P.op("dve", lambda e: e.memset(rb(0, 8 * S), 0.0), writes=rg(0, 8 * S * 2))
            if "gconv" in mixparts:
                for j in range(2):
                    gconv(l, j)
            if "cconv" in mixparts:
                for j in range(2):
                    cconv(l, j)
                cconv_ln(l)
            if "attn" in mixparts:
                for hp in range(4):
                    attention(l, hp)
            Wout = w_out_d[l]
            for dc in range(NCH if "wout" in mixparts else 0):
                jw = wtile(Wout, 0, 8, dc * 128)
                for tt in range(NT):
                    t = tsl(tt)
                    b = nxt("zb", 4)

                    def mm(e, jw=jw, b=b, tt=tt):
                        for yc in range(NCH):
                            i = e.matmul(PS[b][:, :], lhsT=WB[jw][:, yc, :], rhs=rb(yoff(yc, tt), TS),
                                         start=(yc == 0), stop=(yc == NCH - 1))
                        return i
                    yr = []
                    for yc in range(NCH):
                        yr += rg(yoff(yc, tt), TS * 2)
                    P.op("pe", mm, reads=["WB%d" % jw] + yr, writes=ps(b))
                    P.op("dve", lambda e, b=b, dc=dc, t=t: e.tensor_tensor(
                        out=X[:, dc, t], in0=PS[b][:, :], in1=X[:, dc, t], op=ALU.add),
                        reads=ps(b) + [xs(dc, tt)], writes=[xs(dc, tt)])

        def load_x(s):
            for tt in range(NT):
                bi = tt % 2
                off = bi * 16384
                XSv = rf(off, 4 * D).rearrange("p (k d) -> p k d", k=4)
                src = x_d[s, tt * TS:(tt + 1) * TS, :].rearrange("(k p) d -> p k d", p=128)
                P.op("sp", lambda e, XSv=XSv, src=src: e.dma_start(out=XSv, in_=src),
                     writes=rg(off, 16384), dma_key="XS%d" % bi)
                for c in range(NCH):
                    b = nxt("zb", 4)

                    def trs(e, XSv=XSv, c=c, b=b):
                        for k in range(4):
                            i = e.transpose(out=PS[b][:, k * 128:(k + 1) * 128], in_=XSv[:, k, c * 128:(c + 1) * 128],
                                            identity=IDF[:])
                        return i
                    P.op("pe", trs, reads=rg(off, 16384) + ["IDF"], writes=ps(b))
                    if c % 2 == 0:
                        P.op("act", lambda e, b=b, c=c, tt=tt: e.copy(out=X[:, c, tsl(tt)], in_=PS[b][:, :]),
                             reads=ps(b), writes=[xs(c, tt)])
                    else:
                        P.op("dve", lambda e, b=b, c=c, tt=tt: e.tensor_copy(out=X[:, c, tsl(tt)], in_=PS[b][:, :]),
                             reads=ps(b), writes=[xs(c, tt)])

        finals = []

        def store_out(s):
            for tt in range(NT):
                bi = tt % 2
                off = bi * 16384
                OSv = rf(off, 4 * D).rearrange("p (k d) -> p k d", k=4)
                for k in range(4):
                    for hf in range(2):
                        b = nxt("zb", 4)

                        def trs(e, k=k, hf=hf, b=b, tt=tt):
                            for q in range(4):
                                c = hf * 4 + q
                                i = e.transpose(out=PS[b][:, q * 128:(q + 1) * 128],
                                                in_=X[:, c, tt * TS + k * 128: tt * TS + (k + 1) * 128], identity=IDF[:])
                            return i
                        P.op("pe", trs, reads=[xs(hf * 4 + q, tt) for q in range(4)] + ["IDF"], writes=ps(b))
                        wr = rg(off + k * 4096 + hf * 2048, 2048)
                        if (k + hf) % 2 == 0:
                            P.op("act", lambda e, OSv=OSv, k=k, hf=hf, b=b: e.copy(
                                out=OSv[:, k, hf * 512:(hf + 1) * 512], in_=PS[b][:, :]), reads=ps(b), writes=wr)
                        else:
                            P.op("dve", lambda e, OSv=OSv, k=k, hf=hf, b=b: e.tensor_copy(
                                out=OSv[:, k, hf * 512:(hf + 1) * 512], in_=PS[b][:, :]), reads=ps(b), writes=wr)
                dst = out_d[s, tt * TS:(tt + 1) * TS, :].rearrange("(k p) d -> p k d", p=128)
                finals.append(P.op("sp", lambda e, OSv=OSv, dst=dst: e.dma_start(out=dst, in_=OSv),
                                   reads=rg(off, 16384), dma_key="OS%d" % bi))

        for s in range(nseq):
            load_x(s)
            for l in layers:
                if "ffn1" in phases:
                    ffn(l, 0, l * PL + 0)
                if "mixer" in phases:
                    mixer(l)
                if "ffn2" in phases:
                    ffn(l, 1, l * PL + 16)
            if final_norm:
                rmsnorm(2 * PL, list(range(NT)), final=True)
            store_out(s)
        P.emit(nc, final_wait_ops=finals)
    return nc


def _params(inp):
    par = np.zeros((128, NPAR), np.float32)

    def cols(v):
        return np.ascontiguousarray(np.asarray(v, np.float32).reshape(-1, 128).T)
    for l in range(2):
        b = l * PL
        par[:, b + 0:b + 8] = cols(inp["norm_ffn1"][l])
        par[:, b + 8:b + 16] = cols(inp["norm_mix"][l])
        par[:, b + 16:b + 24] = cols(inp["norm_ffn2"][l])
        gw = np.asarray(inp["gconv_w"][l], np.float32)
        cw = np.asarray(inp["cconv_w"][l], np.float32)
        for j in range(2):
            par[:, b + 24 + j * 3: b + 24 + j * 3 + 3] = gw[:, j * 128:(j + 1) * 128].T
            par[:, b + 30 + j * 31: b + 30 + j * 31 + 31] = cw[:, j * 128:(j + 1) * 128].T
        par[:, b + 92:b + 94] = cols(inp["cconv_b"][l])
        par[:, b + 94:b + 96] = cols(inp["cln_g"][l])
        par[:, b + 96:b + 98] = cols(inp["cln_b"][l])
    par[:, 2 * PL:2 * PL + 8] = cols(inp["norm_final"])
    return par


def _masks():
    k = np.arange(128, dtype=np.float64)[:, None]
    q = np.arange(128, dtype=np.float64)[None, :]
    m = np.zeros((128, 12, 256), np.float64)
    NEG = -240000.0
    for tid in range(12):
        c = 2.0 ** (3 - tid)
        m[:, tid, 0:128] = np.where(q >= k, -8.0 * c * (q - k), NEG)
        m[:, tid, 128:256] = np.where(k >= q, -8.0 * c * (128 + q - k), NEG)
    return np.ascontiguousarray(m.reshape(128, 12 * 256).astype(np.float32))


_NC_CACHE = {}


def _get_nc(nseq, layers, final_norm):
    key = (nseq, tuple(layers), final_norm)
    if key not in _NC_CACHE:
        _NC_CACHE[key] = build_program(nseq, list(layers), final_norm)
    return _NC_CACHE[key]


def _run(x_shards, inp, layers, final_norm, n_cores):
    nseq = x_shards[0].shape[0]
    nc = _get_nc(nseq, layers, final_norm)
    common = {k: np.ascontiguousarray(np.asarray(inp[k], np.float32)) for k in
              ("w_in", "w_out", "ffn1_wg", "ffn1_wu", "ffn1_wd", "ffn2_wg", "ffn2_wu", "ffn2_wd")}
    common["params"] = _params(inp)
    common["masks"] = _masks()
    common["ident"] = np.eye(128, dtype=np.float32)
    in_maps = [dict(common, x=np.ascontiguousarray(xs_)) for xs_ in x_shards]
    res = run_bass_kernel_spmd(nc, in_maps, core_ids=list(range(n_cores)))
    return [np.asarray(r["out"]) for r in res.results]


FUSED = True


def kernel(**inputs):
    x = np.asarray(inputs["x"], np.float32)
    shards = [x[i * SEQ_PER_CORE:(i + 1) * SEQ_PER_CORE] for i in range(N_CORES)]
    if FUSED:
        outs = _run(shards, inputs, (0, 1), True, N_CORES)
    else:
        mid = _run(shards, inputs, (0,), False, N_CORES)
        outs = _run(mid, inputs, (1,), True, N_CORES)
    return np.concatenate(outs, axis=0).astype(np.float32)
```
